# Optimizing a Trainium2 kernel written in Bass

```python
import math
import jax
import jax.numpy as jnp
from jax import lax
import numpy as np

D_MODEL = 1024
BATCH = 2
SEQ = 8192
DEPTH = 4

GRID_W = 64
CTX_LEN = 256
N_MIXERS = 3
N_A_LAYERS = (DEPTH + 2) // 3
N_B_LAYERS = (DEPTH + 1) // 3
N_C_LAYERS = DEPTH // 3
D_FF = 2816
N_MOD = 9
RMS_EPS = 1e-6
ROPE_BASE = 10000.0
Q_BLOCK = 128

A_HEADS = 8
A_Q_LORA = 256
A_KV_LORA = 128
A_NOPE = 128
A_ROPE = 64
A_V = 128
A_IN = A_Q_LORA + A_KV_LORA + A_ROPE

B_HEADS = 8
B_HEAD = D_MODEL // (2 * B_HEADS)

C_HEADS = 16
C_HEAD = D_MODEL // C_HEADS
C_WIN_H = 8
C_WIN_W = 16

kernel_name = 'hybrid_mla_diff_natten_macaron_dit'


def rmsnorm(x, g):
    xf = x.astype(jnp.float32)
    y = xf * lax.rsqrt(jnp.mean(xf * xf, axis=-1, keepdims=True) + RMS_EPS)
    return (y * g.astype(jnp.float32)).astype(x.dtype)


def modulation(cond, w_mod, b_mod):
    m = jax.nn.silu(cond) @ w_mod + b_mod
    return m.reshape(cond.shape[0], N_MOD, D_MODEL)


def modulate(x, g, shift, scale):
    return rmsnorm(x, g) * (1.0 + scale[:, None]) + shift[:, None]


def swiglu(h, w_gate, w_up, w_down):
    return (jax.nn.silu(h @ w_gate) * (h @ w_up)) @ w_down


def macaron_half(s, m, base, g, w_gate, w_up, w_down):
    h = modulate(s, g, m[:, base], m[:, base + 1])
    return s + 0.5 * m[:, base + 2, None] * swiglu(h, w_gate, w_up, w_down)


def axial_rope_angles(n_tokens, rot_dim):
    t = jnp.arange(n_tokens, dtype=jnp.int32)
    row = (t // GRID_W).astype(jnp.float32)
    col = (t % GRID_W).astype(jnp.float32)
    axis_dim = rot_dim // 2
    inv = ROPE_BASE ** (-jnp.arange(0, axis_dim, 2, dtype=jnp.float32) / axis_dim)
    ang = jnp.concatenate([row[:, None] * inv, col[:, None] * inv], axis=-1)
    return jnp.cos(ang), jnp.sin(ang)


def apply_rope(x, cos, sin):
    half = x.shape[-1] // 2
    shape = (1, cos.shape[0], 1, half)
    c, s = cos.reshape(shape), sin.reshape(shape)
    xf = x.astype(jnp.float32)
    x1, x2 = xf[..., :half], xf[..., half:]
    return jnp.concatenate([x1 * c - x2 * s, x1 * s + x2 * c], axis=-1).astype(x.dtype)


def attend(q, k, v, scale):
    s = jnp.einsum('bqhd,bkhd->bhqk', q, k) * scale
    p = jax.nn.softmax(s.astype(jnp.float32), axis=-1).astype(v.dtype)
    return jnp.einsum('bhqk,bkhd->bqhd', p, v)


def query_blocks(fn, q):
    b, n = q.shape[:2]
    nb = n // Q_BLOCK
    qb = jnp.moveaxis(q.reshape((b, nb, Q_BLOCK) + q.shape[2:]), 1, 0)
    out = lax.map(fn, qb)
    return jnp.moveaxis(out, 0, 1).reshape((b, n) + out.shape[3:])


def heads(t, n_heads, head_dim):
    return t.reshape(t.shape[0], t.shape[1], n_heads, head_dim)


def qkv_proj(h, w_qkv, need_q):
    if need_q:
        q, k, v = jnp.split(h @ w_qkv, 3, axis=-1)
        return q, k, v
    k, v = jnp.split(h @ w_qkv[:, D_MODEL:], 2, axis=-1)
    return None, k, v


def mla_project(h, w_in, g_q, g_kv, w_uq, w_ukv, rope, need_q):
    b, n, _ = h.shape
    if need_q:
        cq, ckv, k_pe = jnp.split(h @ w_in, [A_Q_LORA, A_Q_LORA + A_KV_LORA], axis=-1)
    else:
        ckv, k_pe = jnp.split(h @ w_in[:, A_Q_LORA:], [A_KV_LORA], axis=-1)
    kv = heads(rmsnorm(ckv, g_kv) @ w_ukv, A_HEADS, A_NOPE + A_V)
    k_pe = k_pe[:, :, None, :]
    if rope is not None:
        k_pe = apply_rope(k_pe, *rope)
    k = jnp.concatenate([kv[..., :A_NOPE], jnp.broadcast_to(k_pe, (b, n, A_HEADS, A_ROPE))], axis=-1)
    v = kv[..., A_NOPE:]
    q = None
    if need_q:
        qh = heads(rmsnorm(cq, g_q) @ w_uq, A_HEADS, A_NOPE + A_ROPE)
        q_pe = qh[..., A_NOPE:]
        if rope is not None:
            q_pe = apply_rope(q_pe, *rope)
        q = jnp.concatenate([qh[..., :A_NOPE], q_pe], axis=-1)
    return q, k, v


def mla_mixer(h_lat, h_ctx, w_in, g_q, g_kv, w_uq, w_ukv, w_out, need_ctx_out):
    b, n, _ = h_lat.shape
    scale = (A_NOPE + A_ROPE) ** -0.5
    rope = axial_rope_angles(n, A_ROPE)
    q_l, k_l, v_l = mla_project(h_lat, w_in, g_q, g_kv, w_uq, w_ukv, rope, True)
    q_c, k_c, v_c = mla_project(h_ctx, w_in, g_q, g_kv, w_uq, w_ukv, None, need_ctx_out)
    k_all = jnp.concatenate([k_c, k_l], axis=1)
    v_all = jnp.concatenate([v_c, v_l], axis=1)
    o_l = query_blocks(lambda qb: attend(qb, k_all, v_all, scale), q_l)
    out_lat = o_l.reshape(b, n, A_HEADS * A_V) @ w_out
    out_ctx = None
    if need_ctx_out:
        o_c = attend(q_c, k_c, v_c, scale)
        out_ctx = o_c.reshape(b, h_ctx.shape[1], A_HEADS * A_V) @ w_out
    return out_lat, out_ctx


def diff_heads(h, w_qkv, rope, need_q):
    q, k, v = qkv_proj(h, w_qkv, need_q)
    k = heads(k, 2 * B_HEADS, B_HEAD)
    v = heads(v, B_HEADS, 2 * B_HEAD)
    if q is not None:
        q = heads(q, 2 * B_HEADS, B_HEAD)
    if rope is not None:
        q = apply_rope(q, *rope)
        k = apply_rope(k, *rope)
    return q, k, v


def diff_attend(q, k, v, lam, scale):
    b, nq = q.shape[:2]
    s = jnp.einsum('bqhd,bkhd->bhqk', q, k) * scale
    p = jax.nn.softmax(s.astype(jnp.float32), axis=-1).reshape(b, B_HEADS, 2, nq, k.shape[1])
    a = (p[:, :, 0] - lam * p[:, :, 1]).astype(v.dtype)
    return jnp.einsum('bhqk,bkhd->bqhd', a, v)


def diff_mixer(h_lat, h_ctx, w_qkv, lq1, lk1, lq2, lk2, g_sub, w_out, lambda_init, need_ctx_out):
    b, n, _ = h_lat.shape
    scale = B_HEAD ** -0.5
    rope = axial_rope_angles(n, B_HEAD)
    q_l, k_l, v_l = diff_heads(h_lat, w_qkv, rope, True)
    q_c, k_c, v_c = diff_heads(h_ctx, w_qkv, None, need_ctx_out)
    f = jnp.float32
    lam = (jnp.exp(jnp.sum(lq1.astype(f) * lk1.astype(f)))
           - jnp.exp(jnp.sum(lq2.astype(f) * lk2.astype(f))) + lambda_init)
    k_all = jnp.concatenate([k_c, k_l], axis=1)
    v_all = jnp.concatenate([v_c, v_l], axis=1)

    def finish(o):
        o = rmsnorm(o, g_sub) * (1.0 - lambda_init)
        return o.reshape(o.shape[0], o.shape[1], D_MODEL) @ w_out

    out_lat = finish(query_blocks(lambda qb: diff_attend(qb, k_all, v_all, lam, scale), q_l))
    out_ctx = finish(diff_attend(q_c, k_c, v_c, lam, scale)) if need_ctx_out else None
    return out_lat, out_ctx


def na_mixer(h_lat, h_ctx, w_qkv, rpb, w_out, need_ctx_out):
    b, n, _ = h_lat.shape
    rows = n // GRID_W
    kh, kw = min(C_WIN_H, rows), C_WIN_W
    scale = C_HEAD ** -0.5
    q, k, v = qkv_proj(h_lat, w_qkv, True)
    grid = (b, rows, GRID_W, C_HEADS, C_HEAD)
    q_g, k_g, v_g = q.reshape(grid), k.reshape(grid), v.reshape(grid)
    q_c, k_c, v_c = qkv_proj(h_ctx, w_qkv, need_ctx_out)
    k_c, v_c = heads(k_c, C_HEADS, C_HEAD), heads(v_c, C_HEADS, C_HEAD)

    col = np.arange(GRID_W)
    col_start = np.clip(col - kw // 2, 0, GRID_W - kw)
    col_idx = col_start[:, None] + np.arange(kw)[None, :]
    col_bias_idx = col_idx - col[:, None] + (C_WIN_W - 1)

    def row_step(args):
        r, q_r = args
        rs = jnp.clip(r - kh // 2, 0, rows - kh)
        k_band = lax.dynamic_slice_in_dim(k_g, rs, kh, axis=1)
        v_band = lax.dynamic_slice_in_dim(v_g, rs, kh, axis=1)
        k_nb = k_band[:, :, col_idx]
        v_nb = v_band[:, :, col_idx]
        s_nb = jnp.einsum('bwhd,biwjhd->bhwij', q_r, k_nb) * scale
        row_bias_idx = rs + jnp.arange(kh, dtype=jnp.int32) - r + (C_WIN_H - 1)
        bias = rpb[:, row_bias_idx][:, :, col_bias_idx]
        s_nb = s_nb + jnp.transpose(bias, (0, 2, 1, 3))[None]
        s_ctx = jnp.einsum('bwhd,bkhd->bhwk', q_r, k_c) * scale
        s = jnp.concatenate([s_nb.reshape(b, C_HEADS, GRID_W, kh * kw), s_ctx], axis=-1)
        p = jax.nn.softmax(s.astype(jnp.float32), axis=-1).astype(v_g.dtype)
        p_nb = p[..., :kh * kw].reshape(b, C_HEADS, GRID_W, kh, kw)
        p_ctx = p[..., kh * kw:]
        return (jnp.einsum('bhwij,biwjhd->bwhd', p_nb, v_nb)
                + jnp.einsum('bhwk,bkhd->bwhd', p_ctx, v_c))

    o = lax.map(row_step, (jnp.arange(rows, dtype=jnp.int32), jnp.moveaxis(q_g, 1, 0)))
    out_lat = jnp.moveaxis(o, 0, 1).reshape(b, n, D_MODEL) @ w_out
    out_ctx = None
    if need_ctx_out:
        o_c = attend(heads(q_c, C_HEADS, C_HEAD), k_c, v_c, scale)
        out_ctx = o_c.reshape(b, h_ctx.shape[1], D_MODEL) @ w_out
    return out_lat, out_ctx


def setup_inputs(seed: int = 0) -> dict:
    key = jax.random.key(seed)
    ks = iter(jax.random.split(key, 32))
    f32 = jnp.float32

    def w(shape, fan_in):
        return jax.random.normal(next(ks), shape, f32) * (fan_in ** -0.5)

    def gain(shape):
        return 1.0 + 0.02 * jax.random.normal(next(ks), shape, f32)

    def small(shape, std):
        return std * jax.random.normal(next(ks), shape, f32)

    D = D_MODEL
    return {
        'x': jax.random.normal(next(ks), (BATCH, SEQ, D), f32),
        'c': jax.random.normal(next(ks), (BATCH, D), f32),
        'ctx': jax.random.normal(next(ks), (BATCH, CTX_LEN, D), f32),
        'c_ctx': jax.random.normal(next(ks), (D,), f32),
        'w_mod': w((DEPTH, D, N_MOD * D), D),
        'b_mod': small((DEPTH, N_MOD * D), 0.02),
        'norm_g': gain((DEPTH, 3, D)),
        'w_ffn_gate': w((DEPTH, 2, D, D_FF), D),
        'w_ffn_up': w((DEPTH, 2, D, D_FF), D),
        'w_ffn_down': w((DEPTH, 2, D_FF, D), D_FF),
        'a_w_in': w((N_A_LAYERS, D, A_IN), D),
        'a_q_norm': gain((N_A_LAYERS, A_Q_LORA)),
        'a_kv_norm': gain((N_A_LAYERS, A_KV_LORA)),
        'a_w_uq': w((N_A_LAYERS, A_Q_LORA, A_HEADS * (A_NOPE + A_ROPE)), A_Q_LORA),
        'a_w_ukv': w((N_A_LAYERS, A_KV_LORA, A_HEADS * (A_NOPE + A_V)), A_KV_LORA),
        'a_w_out': w((N_A_LAYERS, A_HEADS * A_V, D), A_HEADS * A_V),
        'b_w_qkv': w((N_B_LAYERS, D, 3 * D), D),
        'b_lambda_q1': small((N_B_LAYERS, B_HEAD), 0.1),
        'b_lambda_k1': small((N_B_LAYERS, B_HEAD), 0.1),
        'b_lambda_q2': small((N_B_LAYERS, B_HEAD), 0.1),
        'b_lambda_k2': small((N_B_LAYERS, B_HEAD), 0.1),
        'b_subln': gain((N_B_LAYERS, 2 * B_HEAD)),
        'b_w_out': w((N_B_LAYERS, D, D), D),
        'c_w_qkv': w((N_C_LAYERS, D, 3 * D), D),
        'c_rpb': small((N_C_LAYERS, C_HEADS, 2 * C_WIN_H - 1, 2 * C_WIN_W - 1), 0.02),
        'c_w_out': w((N_C_LAYERS, D, D), D),
        'final_g': gain((D,)),
    }


def reference(x, c, ctx, c_ctx, w_mod, b_mod, norm_g, w_ffn_gate, w_ffn_up, w_ffn_down,
              a_w_in, a_q_norm, a_kv_norm, a_w_uq, a_w_ukv, a_w_out,
              b_w_qkv, b_lambda_q1, b_lambda_k1, b_lambda_q2, b_lambda_k2, b_subln, b_w_out,
              c_w_qkv, c_rpb, c_w_out, final_g):
    s_lat, s_ctx = x, ctx
    for i in range(DEPTH):
        need_ctx_out = i < DEPTH - 1
        m_lat = modulation(c, w_mod[i], b_mod[i])
        m_ctx = modulation(c_ctx[None, :], w_mod[i], b_mod[i])
        ffn0 = (w_ffn_gate[i, 0], w_ffn_up[i, 0], w_ffn_down[i, 0])
        ffn1 = (w_ffn_gate[i, 1], w_ffn_up[i, 1], w_ffn_down[i, 1])

        s_lat = macaron_half(s_lat, m_lat, 0, norm_g[i, 0], *ffn0)
        s_ctx = macaron_half(s_ctx, m_ctx, 0, norm_g[i, 0], *ffn0)

        h_lat = modulate(s_lat, norm_g[i, 1], m_lat[:, 3], m_lat[:, 4])
        h_ctx = modulate(s_ctx, norm_g[i, 1], m_ctx[:, 3], m_ctx[:, 4])
        kind, j = i % N_MIXERS, i // N_MIXERS
        if kind == 0:
            o_lat, o_ctx = mla_mixer(h_lat, h_ctx, a_w_in[j], a_q_norm[j], a_kv_norm[j],
                                     a_w_uq[j], a_w_ukv[j], a_w_out[j], need_ctx_out)
        elif kind == 1:
            lambda_init = 0.8 - 0.6 * math.exp(-0.3 * i)
            o_lat, o_ctx = diff_mixer(h_lat, h_ctx, b_w_qkv[j], b_lambda_q1[j], b_lambda_k1[j],
                                      b_lambda_q2[j], b_lambda_k2[j], b_subln[j], b_w_out[j],
                                      lambda_init, need_ctx_out)
        else:
            o_lat, o_ctx = na_mixer(h_lat, h_ctx, c_w_qkv[j], c_rpb[j], c_w_out[j], need_ctx_out)

        s_lat = s_lat + m_lat[:, 5, None] * o_lat
        s_lat = macaron_half(s_lat, m_lat, 6, norm_g[i, 2], *ffn1)
        if need_ctx_out:
            s_ctx = s_ctx + m_ctx[:, 5, None] * o_ctx
            s_ctx = macaron_half(s_ctx, m_ctx, 6, norm_g[i, 2], *ffn1)
    return rmsnorm(s_lat, final_g)
```

```python
import math
import os
from contextlib import ExitStack

import numpy as np
import ml_dtypes
import concourse.bass as bass
import concourse.mybir as mybir
from concourse.bass_utils import run_bass_kernel_spmd

F32 = mybir.dt.float32
BF16 = mybir.dt.bfloat16
AF = mybir.ActivationFunctionType
ALU = mybir.AluOpType

D = 1024
NCH = 8
DFF = 2816
TL = 2048
TC = 256
T = TL + TC
SEQ = 8192
NK = TC + SEQ
NKC = NK // 128
DEPTH = 4
EPS = 1e-6
BLOCKS = [(0, 512), (512, 512), (1024, 512), (1536, 512), (2048, 256)]
HG = 2
NG_FFN = DFF // (128 * HG)

A_HEADS, A_QL, A_KVL, A_NOPE, A_ROPE, A_V = 8, 256, 128, 128, 64, 128
A_IN = A_QL + A_KVL + A_ROPE


def na_deltas(ga):
    if ga == 0:
        return list(range(-2, 4))
    if ga == 15:
        return list(range(-3, 3))
    return list(range(-2, 3))


def na_type(ga):
    return ga if (ga < 3 or ga > 12) else 5


NA_SLOTS = []
for _ga in range(16):
    for _dl in na_deltas(_ga):
        if (na_type(_ga), _dl) not in NA_SLOTS:
            NA_SLOTS.append((na_type(_ga), _dl))


class KB:
    ENGS = ("pe", "act", "dve", "pool", "sp")

    def __init__(self, nc, es):
        self.nc = nc
        self.es = es
        self.prog = {e: [] for e in self.ENGS}
        self.semobj = {e: es.enter_context(nc.semaphore("s_" + e)) for e in self.ENGS}
        self.cnt = {e: 0 for e in self.ENGS}
        self.waited = {e: {} for e in self.ENGS}
        self.lastw = {}
        self.readers = {}
        self.dcnt = {}
        self._pid = {}

    def _deps(self, eng, reads, writes):
        deps = {}
        for k in reads:
            lw = self.lastw.get(k)
            if lw:
                deps[lw[0]] = max(deps.get(lw[0], 0), lw[1])
        for k in writes:
            lw = self.lastw.get(k)
            if lw:
                deps[lw[0]] = max(deps.get(lw[0], 0), lw[1])
            for s, v in self.readers.get(k, {}).items():
                deps[s] = max(deps.get(s, 0), v)
        out = []
        for s, v in deps.items():
            if s in self.dcnt:
                v = self.dcnt[s]
            if self.waited[eng].get(s, 0) < v:
                self.waited[eng][s] = v
                out.append((s, v))
        return out

    def _mark(self, semname, val, reads, writes):
        for k in writes:
            self.lastw[k] = (semname, val)
            self.readers[k] = {}
        for k in reads:
            self.readers.setdefault(k, {})[semname] = val

    def op(self, eng, builders, reads=(), writes=()):
        if callable(builders):
            builders = [builders]
        waits = self._deps(eng, reads, writes)
        self.cnt[eng] += 1
        val = self.cnt[eng]
        semobj = self.semobj

        def thunk(e, waits=waits, builders=builders, sem=semobj[eng]):
            for s, v in waits:
                e.wait_ge(semobj[s], v)
            ins = None
            for b in builders:
                ins = b(e)
            ins.then_inc(sem, 1)
        self.prog[eng].append(thunk)
        self._mark(eng, val, reads, writes)

    def dma(self, q, slot, xfers, reads=(), writes=()):
        if slot not in self.semobj:
            self.semobj[slot] = self.es.enter_context(self.nc.semaphore("d_" + slot))
            self.dcnt[slot] = 0
        waits = self._deps(q, reads, writes)
        self.dcnt[slot] += 16 * len(xfers)
        val = self.dcnt[slot]
        semobj = self.semobj

        def thunk(e, waits=waits, xfers=xfers, slot=slot):
            for s, v in waits:
                e.wait_ge(semobj[s], v)
            for x in xfers:
                e.dma_start(out=x[0], in_=x[1]).then_inc(semobj[slot], 16)
        self.prog[q].append(thunk)
        self._mark(slot, val, reads, writes)

    def pid(self, q, e):
        if q not in self._pid:
            self._pid[q] = e.partition_id()
        return self._pid[q]

    def collective(self, slot, src, dst, reads=(), writes=()):
        if slot not in self.semobj:
            self.semobj[slot] = self.es.enter_context(self.nc.semaphore("c_" + slot))
            self.dcnt[slot] = 0
        waits = self._deps("pool", reads, writes)
        self.dcnt[slot] += 1
        val = self.dcnt[slot]
        semobj = self.semobj

        def thunk(e, waits=waits):
            for s_, v in waits:
                e.wait_ge(semobj[s_], v)
            e.collective_compute("AllGather", ALU.bypass, replica_groups=[list(range(8))],
                                 ins=[src], outs=[dst]).then_inc(semobj[slot], 1)
        self.prog["pool"].append(thunk)
        self._mark(slot, val, reads, writes)

    def barrier(self):
        tot = dict(self.cnt)
        tot.update(self.dcnt)
        semobj = self.semobj
        for e in self.ENGS:
            waits = []
            for s, v in tot.items():
                if s != e and v > 0 and self.waited[e].get(s, 0) < v:
                    self.waited[e][s] = v
                    waits.append((s, v))

            def thunk(en, waits=waits):
                for s, v in waits:
                    en.wait_ge(semobj[s], v)
            self.prog[e].append(thunk)
        self.lastw = {}
        self.readers = {}

    def finish(self):
        nc = self.nc
        names = {"pe": "tensor", "act": "scalar", "dve": "vector", "pool": "gpsimd", "sp": "sync"}
        with nc.Block() as block:
            for e in self.ENGS:
                if self.prog[e]:
                    def body(eng, e=e):
                        for t in self.prog[e]:
                            t(eng)
                    getattr(block, names[e])(body)


class Prog:
    def __init__(self, plan):
        self.plan = plan
        self.nc = bass.Bass("TRN2", target_bir_lowering=False)
        self.es = ExitStack()
        self.kb = KB(self.nc, self.es)
        self.inputs = {}
        self.outputs = {}
        self._din = {}
        nc = self.nc
        base = (nc.sbuf_base + 63) // 64 * 64
        self.base = base
        self.top = nc.sbuf_top
        self._uid = 0
        self.S = self.at("S", [128, NCH, T], F32, 0)
        self.cbase = 73728
        self.H = self.at("H", [128, NCH, T], BF16, 77824)
        self.abase = 114688
        self.hbase = 77824
        o = self.cbase
        self.ONES = self.at("ONES", [128, 128], BF16, o); o += 256
        self.IDENT = self.at("IDENT", [128, 128], BF16, o); o += 256
        self.MODV = self.at("MODV", [128, 72, 2], F32, o); o += 576
        self.GS = self.at("GS", [128, 3, 8, 2], F32, o); o += 192
        self.SH = self.at("SH", [128, 3, 8, 2], F32, o); o += 192
        self.GT = self.at("GT", [128, 3, 8, 2], F32, o); o += 192
        self.CS = self.at("CS", [128, 8, 2], F32, o); o += 64
        self.BM2 = self.at("BM2", [128, 72, 2], F32, o); o += 576
        self.NG2 = self.at("NG2", [128, 3, 8, 2], F32, o); o += 192
        self.MT = self.at("MT", [128, 8, 2], F32, o); o += 64
        self.SMALL = self.at("SMALL", [128, 64], F32, o); o += 256
        assert o <= 77824
        self.PS = [self.es.enter_context(nc.psum_tensor("ps%d" % i, [128, 512], F32)) for i in range(8)]
        self._ffn_wcount = 0

    @staticmethod
    def SK(bi=None, m=None):
        return ["S%d_%d" % (mm, bb) for mm in (range(NCH) if m is None else [m])
                for bb in (range(len(BLOCKS)) if bi is None else [bi])]

    @staticmethod
    def HK(bi=None, k=None):
        return ["H%d_%d" % (kk, bb) for kk in (range(NCH) if k is None else [k])
                for bb in (range(len(BLOCKS)) if bi is None else [bi])]

    def small(self, name, shape, dst, key):
        src = self.din(name, shape, F32)
        nd = len(shape)
        pat = {2: "p a -> p a", 3: "p a b -> p (a b)", 4: "p a b c -> p (a b c)"}[nd]
        flat = src.rearrange(pat) if nd > 2 else src
        dflat = dst.rearrange(pat) if nd > 2 else dst
        self.kb.dma("sp", "ldc", [(dflat, flat)], writes=[key])

    def at(self, name, shape, dtype, off):
        self._uid += 1
        return self.nc.alloc_sbuf_tensor_at("%s_%d" % (name, self._uid), shape, dtype, offset=self.base + off)

    def din(self, name, shape, dtype):
        if name in self._din:
            return self._din[name]
        t = self.nc.dram_tensor(name, list(shape), dtype, kind="ExternalInput").ap()
        self.inputs[name] = (tuple(shape), dtype)
        self._din[name] = t
        return t

    def internal(self, name, shape, dtype):
        return self.nc.dram_tensor(name, list(shape), dtype, kind="Internal").ap()

    def dout(self, name, shape, dtype):
        t = self.nc.dram_tensor(name, list(shape), dtype, kind="ExternalOutput").ap()
        self.outputs[name] = (tuple(shape), dtype)
        return t

    def init_consts(self):
        kb = self.kb
        kb.op("pool", lambda e: e.memset(self.ONES[:], 1.0), writes=["ONES"])

    def load_state(self, name="S_in"):
        src = self.din(name, [D, T], F32)
        kb = self.kb
        v = src.rearrange("(k p) t -> p k t", p=128)
        kb.dma("sp", "ldS", [(self.S[:, k, :], v[:, k, :]) for k in range(NCH)], writes=self.SK())

    def save_state(self, name="S_out"):
        dst = self.dout(name, [D, T], F32)
        kb = self.kb
        v = dst.rearrange("(k p) t -> p k t", p=128)
        kb.dma("sp", "stS", [(v[:, k, :], self.S[:, k, :]) for k in range(NCH)], reads=self.SK(), writes=["S_out"])

    def load_mod(self):
        src = self.din("MOD_in", [128, 144], F32)
        self.kb.dma("sp", "ldm", [(self.MODV[:].rearrange("p a b -> p (a b)"), src[:, :])], writes=["MODV"])

    def save_mod(self):
        dst = self.dout("MOD_out", [128, 144], F32)
        self.kb.dma("sp", "stm", [(dst[:, :], self.MODV[:].rearrange("p a b -> p (a b)"))], reads=["MODV"], writes=["MOD_out"])

    def modulation(self, li):
        kb = self.kb
        kb.barrier()
        wm = self.din("w_mod%d" % li, [D, 9 * D], F32)
        WM = [self.at("WM%d" % i, [128, NCH, 512], F32, self.abase + i * 16384) for i in range(2)]
        self.small("cT%d" % li, [128, 8, 2], self.CS[:], "CS")
        self.small("b_mod%d" % li, [128, 72, 2], self.BM2[:], "BM2")
        kb.op("act", lambda e: e.activation(self.CS[:], self.CS[:], AF.Silu), reads=["CS"], writes=["CS"])
        wv = wm.rearrange("(k p) c -> p k c", p=128)
        for pc in range(18):
            sl = pc % 2
            kb.dma("sp", "ldwm%d" % sl, [(WM[sl][:, k, :], wv[:, k, pc * 512:(pc + 1) * 512]) for k in range(NCH)],
                   writes=["WM%d" % sl])
            bank = self.PS[6 + sl]
            for oc in range(4):
                kb.op("pe", [lambda e, k=k, oc=oc, sl=sl, bank=bank: e.matmul(
                    bank[:, oc * 2:oc * 2 + 2], WM[sl][:, k, oc * 128:(oc + 1) * 128], self.CS[:, k, :],
                    start=(k == 0), stop=(k == NCH - 1), skip_group_check=True) for k in range(NCH)],
                    reads=["WM%d" % sl, "CS"], writes=["ps%d" % (6 + sl)])
            kb.op("dve", lambda e, pc=pc, bank=bank: e.tensor_tensor(
                self.MODV[:, pc * 4:(pc + 1) * 4, :], bank[:, 0:8].rearrange("p (a b) -> p a b", b=2),
                self.BM2[:, pc * 4:(pc + 1) * 4, :], ALU.add),
                reads=["ps%d" % (6 + sl), "BM2"], writes=["MODV"])

    def derive_mod(self, li):
        kb = self.kb
        self.small("norm_g%d" % li, [128, 3, 8, 2], self.NG2[:], "NG2")
        for sub in range(3):
            i0 = 3 * sub
            kb.op("dve", lambda e, i0=i0: e.tensor_scalar(self.MT[:], self.MODV[:, (i0 + 1) * 8:(i0 + 2) * 8, :], 1.0, None, ALU.add),
                  reads=["MODV"], writes=["MT"])
            kb.op("dve", lambda e, sub=sub: e.tensor_tensor(self.GS[:, sub], self.MT[:], self.NG2[:, sub], ALU.mult),
                  reads=["MT", "NG2"], writes=["GS"])
            kb.op("dve", lambda e, sub=sub, i0=i0: e.tensor_copy(self.SH[:, sub], self.MODV[:, i0 * 8:(i0 + 1) * 8, :]),
                  reads=["MODV"], writes=["SH"])
            f = 1.0 if sub == 1 else 0.5
            kb.op("dve", lambda e, sub=sub, i0=i0, f=f: e.tensor_scalar(self.GT[:, sub], self.MODV[:, (i0 + 2) * 8:(i0 + 3) * 8, :], f, None, ALU.mult),
                  reads=["MODV"], writes=["GT"])

    def rms_rstd(self, src_fn, nchunks, n, scale, out_rstd, key_out, sq, sq_key, bank_i, src_keys):
        kb = self.kb
        bank = self.PS[bank_i]
        for k in range(nchunks):
            a = src_fn(k)
            kb.op("act", lambda e, a=a, k=k: e.activation(sq[:a.shape[0], k, :n], a, AF.Square),
                  reads=src_keys, writes=[sq_key + str(k)])
        kb.op("pe", [lambda e, k=k: e.matmul(bank[:, :n], self.ONES[:src_fn(k).shape[0], :], sq[:src_fn(k).shape[0], k, :n],
                                             start=(k == 0), stop=(k == nchunks - 1)) for k in range(nchunks)],
              reads=[sq_key + str(k) for k in range(nchunks)] + ["ONES"], writes=["ps%d" % bank_i])
        kb.op("act", lambda e: e.activation(out_rstd[:, :n], bank[:, :n], AF.Sqrt, bias=self.epsb(), scale=scale),
              reads=["ps%d" % bank_i, "EPSB"], writes=[key_out])
        kb.op("dve", lambda e: e.reciprocal(out_rstd[:, :n], out_rstd[:, :n]), reads=[key_out], writes=[key_out])

    def epsb(self):
        return self.SMALL[:, 0:1]

    def init_small(self):
        self.kb.op("pool", lambda e: e.memset(self.SMALL[:, 0:1], EPS), writes=["EPSB"])
        self.small("sel", [128, 18], self.SMALL[:, 16:34], "SEL")

    def select2(self, dst, np_, A, B, sb, sbk, dkey, gkeys=("GATH",), qa="sp"):
        kb = self.kb
        kb.dma(qa, "ldB" + sbk, [(sb, B)], reads=list(gkeys), writes=[sbk])
        kb.op("dve", lambda e: e.tensor_scalar(dst, dst, self.SMALL[:np_, 16:17], None, ALU.mult), reads=[dkey, "SEL"], writes=[dkey])
        kb.op("dve", lambda e: e.scalar_tensor_tensor(dst, sb, self.SMALL[:np_, 17:18], dst, ALU.mult, ALU.add),
              reads=[dkey, sbk, "SEL"], writes=[dkey])

    def normmod(self, sub, abase_off=0):
        kb = self.kb
        a0 = self.abase + abase_off
        SQ = self.at("SQ", [128, NCH, 512], BF16, a0)
        RS = [self.at("RS%d" % i, [128, 512], F32, a0 + 8192 + i * 2048) for i in range(2)]
        TM = [self.at("TM%d" % i, [128, 512], F32, a0 + 12288 + i * 2048) for i in range(2)]
        for bi, (t0, n) in enumerate(BLOCKS):
            j = 0 if t0 < TL else 1
            rs = RS[bi % 2]
            rk = "RS%d" % (bi % 2)
            kb.op("act", lambda e, t0=t0, n=n: e.activation(SQ[:, :, :n], self.S[:, :, t0:t0 + n], AF.Square),
                  reads=self.SK(bi), writes=["SQ"])
            bank_i = 6 + bi % 2
            bank = self.PS[bank_i]
            kb.op("pe", [lambda e, k=k, n=n, bank=bank: e.matmul(bank[:, :n], self.ONES[:, :], SQ[:, k, :n],
                                                                  start=(k == 0), stop=(k == NCH - 1)) for k in range(NCH)],
                  reads=["SQ", "ONES"], writes=["ps%d" % bank_i])
            kb.op("act", lambda e, n=n, bank=bank, rs=rs: e.activation(rs[:, :n], bank[:, :n], AF.Sqrt, bias=self.epsb(), scale=1.0 / D),
                  reads=["ps%d" % bank_i, "EPSB"], writes=[rk])
            kb.op("dve", lambda e, n=n, rs=rs: e.reciprocal(rs[:, :n], rs[:, :n]), reads=[rk], writes=[rk])
            for k in range(NCH):
                tm = TM[k % 2]
                tk = "TM%d" % (k % 2)
                kb.op("dve", lambda e, k=k, t0=t0, n=n, tm=tm, rs=rs, j=j: e.scalar_tensor_tensor(
                    tm[:, :n], self.S[:, k, t0:t0 + n], self.GS[:, sub, k, j:j + 1], rs[:, :n], ALU.mult, ALU.mult),
                    reads=self.SK(bi, k) + ["GS", rk], writes=[tk])
                kb.op("act", lambda e, k=k, t0=t0, n=n, tm=tm, j=j: e.activation(
                    self.H[:, k, t0:t0 + n], tm[:, :n], AF.Identity, bias=self.SH[:, sub, k, j:j + 1]),
                    reads=[tk, "SH"], writes=self.HK(bi, k))

    def ffn(self, li, f, sub):
        kb = self.kb
        wg = self.din("wg%d_%d" % (li, f), [D, DFF], F32)
        wu = self.din("wu%d_%d" % (li, f), [D, DFF], F32)
        wd = self.din("wd%d_%d" % (li, f), [DFF, D], F32)
        a0 = self.abase + 16384
        W = HG * 128
        WGs = [self.at("WG%d" % i, [128, NCH, W], BF16, a0 + i * 12288) for i in range(3)]
        WUs = [self.at("WU%d" % i, [128, NCH, W], BF16, a0 + i * 12288 + 4096) for i in range(3)]
        WDs = [self.at("WD%d" % i, [128, HG, D], BF16, a0 + i * 12288 + 8192) for i in range(3)]
        a1 = a0 + 36864
        ACTT = [self.at("ACTT%d" % i, [128, HG, T], BF16, a1 + i * HG * T * 2) for i in range(2)]
        a2 = a1 + 2 * HG * T * 2
        SG = [self.at("SG%d" % i, [128, 512], F32, a2 + i * 2048) for i in range(2)]
        assert self.base + a2 + 4096 <= self.top
        wgv = wg.rearrange("(k p) c -> p k c", p=128)
        wuv = wu.rearrange("(k p) c -> p k c", p=128)
        wdv = wd.rearrange("(j p) c -> p j c", p=128)

        def load(g):
            s = g % 3
            kb.dma("pool", "ldw%d" % s,
                   [(WGs[s][:, k, :], wgv[:, k, g * W:(g + 1) * W]) for k in range(NCH)] +
                   [(WUs[s][:, k, :], wuv[:, k, g * W:(g + 1) * W]) for k in range(NCH)] +
                   [(WDs[s][:, jj, :], wdv[:, g * HG + jj, :]) for jj in range(HG)],
                   writes=["W%d" % s])

        cnt = [0]

        def phase_a(g):
            s = g % 3
            at = ACTT[g % 2]
            for jj in range(HG):
                for bi, (t0, n) in enumerate(BLOCKS):
                    c = cnt[0]
                    cnt[0] += 1
                    pg_i, pu_i = (c % 2), 2 + (c % 2)
                    pg, pu = self.PS[pg_i], self.PS[pu_i]
                    sg = SG[c % 2]
                    kb.op("pe", [lambda e, k=k, pg=pg, t0=t0, n=n, jj=jj: e.matmul(
                        pg[:, :n], WGs[s][:, k, jj * 128:(jj + 1) * 128], self.H[:, k, t0:t0 + n],
                        start=(k == 0), stop=(k == NCH - 1)) for k in range(NCH)],
                        reads=["W%d" % s] + self.HK(bi), writes=["ps%d" % pg_i])
                    kb.op("pe", [lambda e, k=k, pu=pu, t0=t0, n=n, jj=jj: e.matmul(
                        pu[:, :n], WUs[s][:, k, jj * 128:(jj + 1) * 128], self.H[:, k, t0:t0 + n],
                        start=(k == 0), stop=(k == NCH - 1)) for k in range(NCH)],
                        reads=["W%d" % s] + self.HK(bi), writes=["ps%d" % pu_i])
                    kb.op("act", lambda e, pg=pg, sg=sg, n=n: e.activation(sg[:, :n], pg[:, :n], AF.Silu),
                          reads=["ps%d" % pg_i], writes=["SG%d" % (c % 2)])
                    kb.op("dve", lambda e, pu=pu, sg=sg, n=n, t0=t0, jj=jj, at=at: e.tensor_tensor(
                        at[:, jj, t0:t0 + n], pu[:, :n], sg[:, :n], ALU.mult),
                        reads=["ps%d" % pu_i, "SG%d" % (c % 2)], writes=["ACTT%d_%d" % (g % 2, bi)])

        dcnt = [0]

        def phase_b(g):
            s = g % 3
            at = ACTT[g % 2]
            for m in range(NCH):
                for bi, (t0, n) in enumerate(BLOCKS):
                    j = 0 if t0 < TL else 1
                    c = dcnt[0]
                    dcnt[0] += 1
                    pd_i = 4 + (c % 2)
                    pd = self.PS[pd_i]
                    kb.op("pe", [lambda e, jj=jj, pd=pd, t0=t0, n=n, m=m: e.matmul(
                        pd[:, :n], WDs[s][:, jj, m * 128:(m + 1) * 128], at[:, jj, t0:t0 + n],
                        start=(jj == 0), stop=(jj == HG - 1)) for jj in range(HG)],
                        reads=["W%d" % s, "ACTT%d_%d" % (g % 2, bi)], writes=["ps%d" % pd_i])
                    kb.op("dve", lambda e, pd=pd, t0=t0, n=n, m=m, j=j: e.scalar_tensor_tensor(
                        self.S[:, m, t0:t0 + n], pd[:, :n], self.GT[:, sub, m, j:j + 1], self.S[:, m, t0:t0 + n],
                        ALU.mult, ALU.add),
                        reads=["ps%d" % pd_i, "GT", "S%d_%d" % (m, bi)], writes=["S%d_%d" % (m, bi)])

        load(0)
        load(1)
        load(2)
        phase_a(0)
        for g in range(1, NG_FFN):
            phase_a(g)
            phase_b(g - 1)
            if g + 2 < NG_FFN:
                load(g + 2)
        phase_b(NG_FFN - 1)

    def attn(self, nq, chunks, streams, scale):
        kb = self.kb

        def scores(ci):
            c = chunks[ci]
            for s in streams:
                b = s["st"][ci % 2]
                mm = s["score"](c)
                kb.op("pe", [lambda e, i=i, b=b, mm=mm: e.matmul(self.PS[b][:, :nq], mm[i][0], mm[i][1],
                                                                start=(i == 0), stop=(i == len(mm) - 1))
                             for i in range(len(mm))],
                      reads=s["reads"], writes=["ps%d" % b])

        def expo(ci):
            for s in streams:
                b = s["st"][ci % 2]
                pt = s["pt"][ci % 2]
                kb.op("act", lambda e, b=b, pt=pt: e.activation(pt[:, :nq], self.PS[b][:, :nq], AF.Exp, scale=scale),
                      reads=["ps%d" % b], writes=[s["ptk"][ci % 2]])

        def pv(ci):
            c = chunks[ci]
            first, last = ci == 0, ci == len(chunks) - 1
            for s in streams:
                pt = s["pt"][ci % 2]
                M = s["M"]
                v = s["v"](c)
                kb.op("pe", [lambda e, pt=pt, M=M, v=v, s=s: e.matmul(self.PS[s["o"]][:M, :nq], v, pt[:, :nq], start=first, stop=last),
                             lambda e, pt=pt, M=M, s=s: e.matmul(self.PS[s["l"]][:M, :nq], self.ONES[:, :M], pt[:, :nq], start=first, stop=last)],
                      reads=[s["ptk"][ci % 2], "ONES"] + s["vreads"], writes=["ps%d" % s["o"], "ps%d" % s["l"]])

        scores(0)
        for ci in range(len(chunks)):
            if ci + 1 < len(chunks):
                scores(ci + 1)
            expo(ci)
            pv(ci)

    def load_tables(self, off):
        COS = self.at("COS", [128, TL], F32, off)
        SIN = self.at("SIN", [128, TL], F32, off + 8192)
        c = self.din("cos4", [128, TL], F32)
        s_ = self.din("sinm4", [128, TL], F32)
        self.kb.dma("sp", "ldt", [(COS[:], c[:, :]), (SIN[:], s_[:, :])], writes=["TAB"])
        return COS, SIN

    def mla_kv(self, j):
        kb = self.kb
        kb.barrier()
        a0 = self.abase
        w_in = self.din("a_w_in%d" % j, [D, A_IN], F32)
        xin = self.internal("mla_xin%d" % j, [192, TL], BF16)
        xctx = self.internal("mla_xctx%d" % j, [192, TC], BF16)
        G = self.internal("mla_G%d" % j, [8 * 192, TL], BF16)
        self.mla_x = (xctx, G)
        WIN = self.at("WIN", [128, NCH, 256], BF16, a0)
        COS, SIN = self.load_tables(a0 + 4096)
        XA = self.at("XA", [128, 512], F32, a0 + 20480)
        SQ = self.at("SQk", [128, 512], BF16, a0 + 22528)
        RS = self.at("RSk", [128, 512], F32, a0 + 23552)
        T1 = self.at("T1", [64, 512], F32, a0 + 25600)
        T2 = self.at("T2", [64, 512], F32, a0 + 27648)
        OKV = self.at("OKV", [128, T], BF16, a0 + 29696)
        OPE = self.at("OPE", [64, T], BF16, a0 + 34304)
        GKV = self.SMALL[:, 1:2]
        self.small("a_kv_norm%d" % j, [128, 1], GKV, "GKV")
        wv = w_in.rearrange("(k p) c -> p k c", p=128)
        kb.dma("pool", "ldwin", [(WIN[:, k, 0:192], wv[:, k, 256:448]) for k in range(NCH)], writes=["WIN"])
        kb.op("dve", lambda e: e.tensor_copy(WIN[:, :, 192:224], WIN[:, :, 160:192]), reads=["WIN"], writes=["WINs1"])
        kb.op("dve", lambda e: e.tensor_copy(WIN[:, :, 224:256], WIN[:, :, 128:160]), reads=["WIN"], writes=["WINs2"])
        def blk(bi, t0, n):
            pa, pb, pc, pd = bi % 2, 2 + bi % 2, 4 + bi % 2, 6 + bi % 2
            for (bank, c0, c1, M) in ((pa, 0, 128, 128), (pb, 128, 192, 64), (pc, 192, 256, 64)):
                kb.op("pe", [lambda e, k=k, bank=bank, c0=c0, c1=c1, M=M: e.matmul(
                    self.PS[bank][:M, :n], WIN[:, k, c0:c1], self.H[:, k, t0:t0 + n], start=(k == 0), stop=(k == NCH - 1))
                    for k in range(NCH)], reads=["WIN", "WINs1", "WINs2"] + self.HK(bi), writes=["ps%d" % bank])
            kb.op("act", lambda e: e.activation(XA[:, :n], self.PS[pa][:, :n], AF.Copy), reads=["ps%d" % pa], writes=["XA"])
            kb.op("act", lambda e: e.activation(SQ[:, :n], self.PS[pa][:, :n], AF.Square), reads=["ps%d" % pa], writes=["SQk"])
            kb.op("pe", lambda e: e.matmul(self.PS[pd][:, :n], self.ONES[:, :], SQ[:, :n], start=True, stop=True),
                  reads=["SQk", "ONES"], writes=["ps%d" % pd])
            kb.op("act", lambda e: e.activation(RS[:, :n], self.PS[pd][:, :n], AF.Sqrt, bias=self.epsb(), scale=1.0 / A_KVL),
                  reads=["ps%d" % pd, "EPSB"], writes=["RSk"])
            kb.op("dve", lambda e: e.reciprocal(RS[:, :n], RS[:, :n]), reads=["RSk"], writes=["RSk"])
            kb.op("dve", lambda e: e.scalar_tensor_tensor(OKV[:, t0:t0 + n], XA[:, :n], GKV, RS[:, :n], ALU.mult, ALU.mult),
                  reads=["XA", "RSk", "GKV"], writes=["OKV%d" % bi])
            if t0 < TL:
                kb.op("dve", lambda e: e.tensor_tensor(T1[:, :n], self.PS[pb][:64, :n], COS[:64, t0:t0 + n], ALU.mult),
                      reads=["ps%d" % pb, "TAB"], writes=["T1"])
                kb.op("dve", lambda e: e.tensor_tensor(T2[:, :n], self.PS[pc][:64, :n], SIN[:64, t0:t0 + n], ALU.mult),
                      reads=["ps%d" % pc, "TAB"], writes=["T2"])
                kb.op("pool", lambda e: e.tensor_tensor(OPE[:, t0:t0 + n], T1[:, :n], T2[:, :n], ALU.add),
                      reads=["T1", "T2"], writes=["OPE%d" % bi])
            else:
                kb.op("act", lambda e: e.activation(OPE[:, t0:t0 + n], self.PS[pb][:64, :n], AF.Copy),
                      reads=["ps%d" % pb], writes=["OPE%d" % bi])
        for bi, (t0, n) in enumerate(BLOCKS):
            blk(bi, t0, n)
        kb.dma("sp", "stkv", [(xin[0:128, :], OKV[:, 0:TL]), (xin[128:192, :], OPE[:, 0:TL]),
                              (xctx[0:128, :], OKV[:, TL:T]), (xctx[128:192, :], OPE[:, TL:T])],
               reads=["OKV%d" % i for i in range(5)] + ["OPE%d" % i for i in range(5)], writes=["XIN"])
        kb.collective("ag", xin, G, reads=["XIN"], writes=["GATH"])

    def mla_attn(self, j, need_ctx):
        kb = self.kb
        kb.barrier()
        a0 = self.abase
        h0 = self.hbase
        w_in = self.din("a_w_in%d" % j, [D, A_IN], F32)
        w_uq = self.din("a_w_uq%d" % j, [A_QL, A_HEADS * 192], F32)
        w_ukv = self.din("a_w_ukv%d" % j, [A_KVL, A_HEADS * 256], F32)
        w_out = self.din("a_w_out%d" % j, [D, D], F32)
        xctx, G = self.mla_x
        CQN = self.at("CQN", [128, 2, T], BF16, h0 + 106496)
        WINQ = self.at("WINQ", [128, NCH, 256], BF16, a0)
        XAQ = self.at("XAQ", [128, 2, 512], F32, a0 + 4096)
        SQ = self.at("SQq", [128, 2, 512], BF16, a0 + 8192)
        RS = self.at("RSq", [128, 512], F32, a0 + 10240)
        GQ = self.SMALL[:, 2:4]
        self.small("a_q_norm%d" % j, [128, 2], GQ, "GQ")
        wv = w_in.rearrange("(k p) c -> p k c", p=128)
        kb.dma("pool", "ldwin", [(WINQ[:, k, :], wv[:, k, 0:256]) for k in range(NCH)], writes=["WINQ"])
        def qproj(bi, t0, n):
            pd = 6 + bi % 2
            for c in range(2):
                bank = 2 * (bi % 2) + c
                kb.op("pe", [lambda e, k=k, bank=bank, c=c: e.matmul(
                    self.PS[bank][:, :n], WINQ[:, k, c * 128:(c + 1) * 128], self.H[:, k, t0:t0 + n], start=(k == 0), stop=(k == NCH - 1))
                    for k in range(NCH)], reads=["WINQ"] + self.HK(bi), writes=["ps%d" % bank])
                kb.op("act", lambda e, bank=bank, c=c: e.activation(XAQ[:, c, :n], self.PS[bank][:, :n], AF.Copy),
                      reads=["ps%d" % bank], writes=["XAQ%d" % c])
                kb.op("act", lambda e, bank=bank, c=c: e.activation(SQ[:, c, :n], self.PS[bank][:, :n], AF.Square),
                      reads=["ps%d" % bank], writes=["SQq%d" % c])
            kb.op("pe", [lambda e, c=c: e.matmul(self.PS[pd][:, :n], self.ONES[:, :], SQ[:, c, :n], start=(c == 0), stop=(c == 1))
                         for c in range(2)], reads=["SQq0", "SQq1", "ONES"], writes=["ps%d" % pd])
            kb.op("act", lambda e: e.activation(RS[:, :n], self.PS[pd][:, :n], AF.Sqrt, bias=self.epsb(), scale=1.0 / A_QL),
                  reads=["ps%d" % pd, "EPSB"], writes=["RSq"])
            kb.op("dve", lambda e: e.reciprocal(RS[:, :n], RS[:, :n]), reads=["RSq"], writes=["RSq"])
            for c in range(2):
                kb.op("dve", lambda e, c=c: e.scalar_tensor_tensor(CQN[:, c, t0:t0 + n], XAQ[:, c, :n], GQ[:, c:c + 1], RS[:, :n],
                                                                   ALU.mult, ALU.mult),
                      reads=["XAQ%d" % c, "RSq", "GQ"], writes=["CQN"])
        for bi, (t0, n) in enumerate(BLOCKS):
            if t0 < TL or need_ctx:
                qproj(bi, t0, n)
        kb.barrier()
        CKV = self.at("CKV", [128, NK], BF16, h0)
        KPE = self.at("KPE", [64, NK], BF16, h0 + 16896)
        KT = self.at("KT", [128, NK], BF16, h0 + 33792)
        V = self.at("V", [128, NKC, 128], BF16, h0 + 50688)
        COS, SIN = self.load_tables(h0 + 67584)
        WUQ = [self.at("WUQ%d" % i, [128, 2, 256], BF16, h0 + 83968 + i * 3584) for i in range(2)]
        WUKV = [self.at("WUKV%d" % i, [128, 256], BF16, h0 + 83968 + i * 3584 + 1024) for i in range(2)]
        WOUT = [self.at("WOUT%d" % i, [128, D], BF16, h0 + 83968 + i * 3584 + 1536) for i in range(2)]
        QN = [self.at("QN%d" % i, [128, 512], BF16, h0 + 91136 + i * 1024) for i in range(2)]
        QPE = [self.at("QPE%d" % i, [64, 512], BF16, h0 + 93184 + i * 1024) for i in range(2)]
        PT = [self.at("PT%d" % i, [128, 512], BF16, h0 + 95232 + i * 1024) for i in range(2)]
        ON = [self.at("ON%d" % i, [128, 512], BF16, h0 + 98304 + i * 1024) for i in range(2)]
        RL = self.at("RL", [128, 512], F32, h0 + 100352)
        T1 = self.at("T1", [64, 512], F32, h0 + 102400)
        T2 = self.at("T2", [64, 512], F32, h0 + 104448)
        G3 = G.rearrange("(r f) t -> r f t", r=8)
        SB = [self.at("SB%d" % i, [128, TL], BF16, h0 + 115712 + i * 4096) for i in range(2)]
        kb.dma("sp", "ldkv", [(CKV[:, 0:TC], xctx[0:128, :]), (KPE[:, 0:TC], xctx[128:192, :])] +
               [(CKV[:, TC + q * TL:TC + (q + 1) * TL], G3[q, 0:128, :]) for q in range(4)] +
               [(KPE[:, TC + q * TL:TC + (q + 1) * TL], G3[q, 128:192, :]) for q in range(4)],
               reads=["XIN", "GATH"], writes=["CKV", "KPE"] + ["CKV%d" % q for q in range(4)] + ["KPE%d" % q for q in range(4)])
        for q in range(4):
            self.select2(CKV[:, TC + q * TL:TC + (q + 1) * TL], 128, None, G3[4 + q, 0:128, :], SB[0][:, :], "SB0", "CKV%d" % q)
            self.select2(KPE[:, TC + q * TL:TC + (q + 1) * TL], 64, None, G3[4 + q, 128:192, :], SB[1][:64, :], "SB1", "KPE%d" % q)
        ckv_keys = ["CKV"] + ["CKV%d" % q for q in range(4)]
        kpe_keys = ["KPE"] + ["KPE%d" % q for q in range(4)]
        uqv = w_uq.rearrange("(k p) c -> p k c", p=128)
        scale = float((A_NOPE + A_ROPE) ** -0.5)
        qblocks = [b for b in enumerate(BLOCKS) if b[1][0] < TL or need_ctx]
        it = [0]

        def head(h):
            ws = h % 2
            kb.dma("pool", "ldwh%d" % ws,
                   [(WUQ[ws][:, kk, 0:192], uqv[:, kk, h * 192:(h + 1) * 192]) for kk in range(2)] +
                   [(WUKV[ws][:, :], w_ukv[:, h * 256:(h + 1) * 256]), (WOUT[ws][:, :], w_out[h * 128:(h + 1) * 128, :])],
                   writes=["WH%d" % ws])
            kb.op("dve", lambda e, ws=ws: e.tensor_copy(WUQ[ws][:, :, 192:224], WUQ[ws][:, :, 160:192]), reads=["WH%d" % ws], writes=["WHs1_%d" % ws])
            kb.op("dve", lambda e, ws=ws: e.tensor_copy(WUQ[ws][:, :, 224:256], WUQ[ws][:, :, 128:160]), reads=["WH%d" % ws], writes=["WHs2_%d" % ws])
            whk = ["WH%d" % ws, "WHs1_%d" % ws, "WHs2_%d" % ws]
            for kc in range((NK + 511) // 512):
                k0 = kc * 512
                n = min(512, NK - k0)
                bank = 6 + kc % 2
                kb.op("pe", lambda e, bank=bank, k0=k0, n=n: e.matmul(self.PS[bank][:, :n], WUKV[ws][:, 0:128], CKV[:, k0:k0 + n], start=True, stop=True),
                      reads=whk + ckv_keys, writes=["ps%d" % bank])
                if kc % 2 == 0:
                    kb.op("act", lambda e, bank=bank, k0=k0, n=n: e.activation(KT[:, k0:k0 + n], self.PS[bank][:, :n], AF.Copy),
                          reads=["ps%d" % bank], writes=["KT%d" % kc])
                else:
                    kb.op("dve", lambda e, bank=bank, k0=k0, n=n: e.tensor_copy(KT[:, k0:k0 + n], self.PS[bank][:, :n]),
                          reads=["ps%d" % bank], writes=["KT%d" % kc])
            for g in range((NKC + 3) // 4):
                cs = list(range(4 * g, min(4 * g + 4, NKC)))
                bank = 6 + g % 2
                kb.op("pe", [lambda e, bank=bank, i=i, c=c: e.matmul(self.PS[bank][:, i * 128:(i + 1) * 128], CKV[:, c * 128:(c + 1) * 128],
                                                                    WUKV[ws][:, 128:256], start=True, stop=True, skip_group_check=True)
                             for i, c in enumerate(cs)], reads=whk + ckv_keys, writes=["ps%d" % bank])
                w = 128 * len(cs)
                dst = V[:, cs[0]:cs[0] + len(cs), :].rearrange("p a b -> p (a b)")
                if g % 2 == 0:
                    kb.op("dve", lambda e, bank=bank, dst=dst, w=w: e.tensor_copy(dst, self.PS[bank][:, :w]), reads=["ps%d" % bank], writes=["V%d" % g])
                else:
                    kb.op("act", lambda e, bank=bank, dst=dst, w=w: e.activation(dst, self.PS[bank][:, :w], AF.Copy), reads=["ps%d" % bank], writes=["V%d" % g])
            def qb(bi, t0, n):
                jm = 0 if t0 < TL else 1
                sl = it[0] % 2
                it[0] += 1
                qn, qpe, on = QN[sl], QPE[sl], ON[sl]
                kb.op("pe", [lambda e, kk=kk: e.matmul(self.PS[6][:, :n], WUQ[ws][:, kk, 0:128], CQN[:, kk, t0:t0 + n], start=(kk == 0), stop=(kk == 1))
                             for kk in range(2)], reads=whk + ["CQN"], writes=["ps6"])
                kb.op("act", lambda e, qn=qn: e.activation(qn[:, :n], self.PS[6][:, :n], AF.Copy), reads=["ps6"], writes=["QN%d" % sl])
                kb.op("pe", [lambda e, kk=kk: e.matmul(self.PS[7][:64, :n], WUQ[ws][:, kk, 128:192], CQN[:, kk, t0:t0 + n], start=(kk == 0), stop=(kk == 1))
                             for kk in range(2)], reads=whk + ["CQN"], writes=["ps7"])
                if t0 < TL:
                    kb.op("dve", lambda e: e.tensor_tensor(T1[:, :n], self.PS[7][:64, :n], COS[:64, t0:t0 + n], ALU.mult),
                          reads=["ps7", "TAB"], writes=["T1"])
                    kb.op("pe", [lambda e, kk=kk: e.matmul(self.PS[7][:64, :n], WUQ[ws][:, kk, 192:256], CQN[:, kk, t0:t0 + n], start=(kk == 0), stop=(kk == 1))
                                 for kk in range(2)], reads=whk + ["CQN"], writes=["ps7"])
                    kb.op("dve", lambda e: e.tensor_tensor(T2[:, :n], self.PS[7][:64, :n], SIN[:64, t0:t0 + n], ALU.mult),
                          reads=["ps7", "TAB"], writes=["T2"])
                    kb.op("pool", lambda e, qpe=qpe: e.tensor_tensor(qpe[:, :n], T1[:, :n], T2[:, :n], ALU.add),
                          reads=["T1", "T2"], writes=["QPE%d" % sl])
                else:
                    kb.op("act", lambda e, qpe=qpe: e.activation(qpe[:, :n], self.PS[7][:64, :n], AF.Copy), reads=["ps7"], writes=["QPE%d" % sl])
                ob, lb = 2 + 2 * sl, 3 + 2 * sl
                stream = dict(
                    score=lambda c, qn=qn, qpe=qpe: [(KT[:, c * 128:(c + 1) * 128], qn[:, :n]), (KPE[:, c * 128:(c + 1) * 128], qpe[:, :n])],
                    v=lambda c: V[:, c, :], M=128, st=(0, 1), o=ob, l=lb, pt=(PT[0], PT[1]), ptk=("PT0", "PT1"),
                    reads=["KT%d" % i for i in range(17)] + kpe_keys + ["QN%d" % sl, "QPE%d" % sl], vreads=["V%d" % i for i in range(17)])
                chunks = list(range(NKC)) if t0 < TL else [0, 1]
                self.attn(n, chunks, [stream], scale)
                kb.op("dve", lambda e, lb=lb: e.reciprocal(RL[:, :n], self.PS[lb][:, :n]), reads=["ps%d" % lb], writes=["RL"])
                kb.op("dve", lambda e, ob=ob, on=on: e.tensor_tensor(on[:, :n], self.PS[ob][:, :n], RL[:, :n], ALU.mult),
                      reads=["ps%d" % ob, "RL"], writes=["ON%d" % sl])
                for m in range(NCH):
                    bank = 6 + m % 2
                    kb.op("pe", lambda e, bank=bank, m=m, on=on: e.matmul(self.PS[bank][:, :n], WOUT[ws][:, m * 128:(m + 1) * 128], on[:, :n], start=True, stop=True),
                          reads=whk + ["ON%d" % sl], writes=["ps%d" % bank])
                    kb.op("dve", lambda e, bank=bank, m=m: e.scalar_tensor_tensor(
                        self.S[:, m, t0:t0 + n], self.PS[bank][:, :n], self.GT[:, 1, m, jm:jm + 1], self.S[:, m, t0:t0 + n], ALU.mult, ALU.add),
                        reads=["ps%d" % bank, "GT"] + self.SK(bi, m), writes=self.SK(bi, m))
            for bi, (t0, n) in qblocks:
                qb(bi, t0, n)
        for h in range(A_HEADS):
            head(h)
        kb.barrier()

    def qkv_kv(self, wname, rope):
        kb = self.kb
        kb.barrier()
        a0 = self.abase
        w = self.din(wname, [D, 3 * D], F32)
        tag = wname[0]
        X = self.internal(tag + "_x", [2 * TL, D], BF16)
        kt_lat = X[0:TL, :].rearrange("(f a) b -> f (a b)", a=2)
        v_lat = X[TL:2 * TL, :]
        kt_ctx = self.internal(tag + "_kt_ctx", [D, TC], BF16)
        v_ctx = self.internal(tag + "_v_ctx", [TC, D], BF16)
        GK = self.internal(tag + "_G", [8 * 2 * TL, D], BF16)
        GV = X
        self.pair_x = (kt_ctx, v_ctx, GK, GV, kt_lat, v_lat)
        WK = self.at("WK", [128, NCH, D], BF16, a0)
        WKS = self.at("WKS", [128, NCH, D], BF16, a0 + 16384)
        if rope:
            COS, SIN = self.load_tables(a0 + 32768)
        T1 = self.at("T1", [128, 512], F32, a0 + 49152)
        T2 = self.at("T2", [128, 512], F32, a0 + 51200)
        OK_ = [self.at("OK%d" % i, [128, 512], BF16, a0 + 53248 + i * 1024) for i in range(2)]
        WV = self.at("WV", [128, NCH, D], BF16, a0 + 55296)
        VO = [self.at("VO%d" % i, [128, D], BF16, a0 + 71680 + i * 2048) for i in range(2)]
        wv = w.rearrange("(k p) c -> p k c", p=128)
        kb.dma("pool", "ldwk", [(WK[:, k, :], wv[:, k, D:2 * D]) for k in range(NCH)], writes=["WK"])
        kb.dma("pool", "ldwv", [(WV[:, k, :], wv[:, k, 2 * D:3 * D]) for k in range(NCH)], writes=["WV"])
        if rope:
            for k in range(NCH):
                src = WK[:, k, :].rearrange("p (h t d) -> p h t d", t=2, d=32)
                dst = WKS[:, k, :].rearrange("p (h t d) -> p h t d", t=2, d=32)
                kb.op("pool", lambda e, src=src, dst=dst: e.tensor_copy(dst[:, :, 0, :], src[:, :, 1, :]), reads=["WK"], writes=["WKSa%d" % k])
                kb.op("pool", lambda e, src=src, dst=dst: e.tensor_copy(dst[:, :, 1, :], src[:, :, 0, :]), reads=["WK"], writes=["WKSb%d" % k])
        wks_keys = ["WKSa%d" % k for k in range(NCH)] + ["WKSb%d" % k for k in range(NCH)] if rope else []
        it = [0]

        def kblk(m, bi, t0, n):
            i = it[0]
            it[0] += 1
            pa, pb = i % 2, 2 + i % 2
            ok = OK_[i % 2]
            kb.op("pe", [lambda e, k=k: e.matmul(self.PS[pa][:, :n], WK[:, k, m * 128:(m + 1) * 128], self.H[:, k, t0:t0 + n],
                                                 start=(k == 0), stop=(k == NCH - 1)) for k in range(NCH)],
                  reads=["WK"] + self.HK(bi), writes=["ps%d" % pa])
            if rope and t0 < TL:
                kb.op("pe", [lambda e, k=k: e.matmul(self.PS[pb][:, :n], WKS[:, k, m * 128:(m + 1) * 128], self.H[:, k, t0:t0 + n],
                                                     start=(k == 0), stop=(k == NCH - 1)) for k in range(NCH)],
                      reads=wks_keys + self.HK(bi), writes=["ps%d" % pb])
                kb.op("dve", lambda e: e.tensor_tensor(T1[:, :n], self.PS[pa][:, :n], COS[:, t0:t0 + n], ALU.mult), reads=["ps%d" % pa, "TAB"], writes=["T1"])
                kb.op("dve", lambda e: e.tensor_tensor(T2[:, :n], self.PS[pb][:, :n], SIN[:, t0:t0 + n], ALU.mult), reads=["ps%d" % pb, "TAB"], writes=["T2"])
                kb.op("pool", lambda e: e.tensor_tensor(ok[:, :n], T1[:, :n], T2[:, :n], ALU.add), reads=["T1", "T2"], writes=["OK%d" % (i % 2)])
            else:
                kb.op("act", lambda e: e.activation(ok[:, :n], self.PS[pa][:, :n], AF.Copy), reads=["ps%d" % pa], writes=["OK%d" % (i % 2)])
            dst = kt_lat[m * 128:(m + 1) * 128, t0:t0 + n] if t0 < TL else kt_ctx[m * 128:(m + 1) * 128, :]
            kb.dma("sp", "stk%d" % (i % 2), [(dst, ok[:, :n])], reads=["OK%d" % (i % 2)], writes=["KT_out%d" % (i % 2)])

        for m in range(NCH):
            for bi, (t0, n) in enumerate(BLOCKS):
                kblk(m, bi, t0, n)

        def vblk(tc):
            vo = VO[tc % 2]
            bi = min(tc // 4, 4)
            for half in range(2):
                bank = 4 + half + 2 * (tc % 2)
                kb.op("pe", [lambda e, k=k, bank=bank, half=half: e.matmul(self.PS[bank][:, :], self.H[:, k, tc * 128:(tc + 1) * 128],
                                                                         WV[:, k, half * 512:(half + 1) * 512], start=(k == 0), stop=(k == NCH - 1))
                             for k in range(NCH)], reads=["WV"] + self.HK(bi), writes=["ps%d" % bank])
                if half == 0:
                    kb.op("act", lambda e, bank=bank: e.activation(vo[:, 0:512], self.PS[bank][:, :], AF.Copy), reads=["ps%d" % bank], writes=["VO%d_0" % (tc % 2)])
                else:
                    kb.op("dve", lambda e, bank=bank: e.tensor_copy(vo[:, 512:1024], self.PS[bank][:, :]), reads=["ps%d" % bank], writes=["VO%d_1" % (tc % 2)])
            dst = v_lat[tc * 128:(tc + 1) * 128, :] if tc < 16 else v_ctx[(tc - 16) * 128:(tc - 15) * 128, :]
            kb.dma("sp", "stv%d" % (tc % 2), [(dst, vo[:, :])], reads=["VO%d_0" % (tc % 2), "VO%d_1" % (tc % 2)], writes=["V_out%d" % (tc % 2)])

        for tc in range(T // 128):
            vblk(tc)
        kb.collective("agk", X, GK, reads=["KT_out0", "KT_out1", "V_out0", "V_out1"], writes=["GATHK", "GATHV"])

    def pair_attn(self, kind, li, need_ctx):
        kb = self.kb
        kb.barrier()
        a0 = self.abase
        diff = kind == "b"
        nk = NK if diff else TC + 2560
        nkc = nk // 128
        w = self.din("%s_w_qkv" % kind, [D, 3 * D], F32)
        w_out = self.din("%s_w_out" % kind, [D, D], F32)
        kt_ctx, v_ctx, GK, GV, kt_lat, v_lat = self.pair_x
        o = a0
        KT = self.at("KT", [128, nk], BF16, o); o += nk * 2
        V = self.at("V", [128, nkc, 128], BF16, o); o += nk * 2
        WQ = [self.at("WQ%d" % i, [128, NCH, 128], BF16, o + i * 2048) for i in range(2)]; o += 4096
        WQS = [self.at("WQS%d" % i, [128, NCH, 128], BF16, o + i * 2048) for i in range(2)]; o += 4096
        WOUT = [self.at("WOUT%d" % i, [128, D], BF16, o + i * 2048) for i in range(2)]; o += 4096
        QT = self.at("QT", [128, T], BF16, o); o += T * 2
        PT = [self.at("PT%d" % i, [128, 512], BF16, o + i * 1024) for i in range(4)]; o += 4096
        F = [self.at("F%d" % i, [128, 512], F32, o + i * 2048) for i in range(4)]; o += 8192
        ON = [self.at("ON%d" % i, [128, 512], BF16, o + i * 1024) for i in range(2)]; o += 2048
        SQd = self.at("SQd", [128, 512], BF16, o); o += 1024
        if diff:
            COS, SIN = self.load_tables(o); o += 16384
            LAM = self.at("LAM", [128, 4, 64], F32, o); o += 1024
            TL2 = self.at("TL2", [128, 2, 64], F32, o); o += 512
            self.small("b_lam", [128, 4, 64], LAM[:], "LAM")
            GSUB = self.SMALL[:, 4:5]
            self.small("b_subln", [128, 1], GSUB, "GSUBr")
            lam_init = 0.8 - 0.6 * math.exp(-0.3 * li)
            S1, S2, NLAM, GSL = self.SMALL[:, 8:9], self.SMALL[:, 9:10], self.SMALL[:, 10:11], self.SMALL[:, 11:12]
            for i, sx in enumerate((S1, S2)):
                kb.op("dve", lambda e, i=i: e.tensor_tensor(TL2[:, i, :], LAM[:, 2 * i, :], LAM[:, 2 * i + 1, :], ALU.mult), reads=["LAM"], writes=["TL2_%d" % i])
                kb.op("dve", lambda e, i=i, sx=sx: e.tensor_reduce(sx, TL2[:, i, :], mybir.AxisListType.X, ALU.add), reads=["TL2_%d" % i], writes=["LS%d" % i])
                kb.op("act", lambda e, sx=sx: e.activation(sx, sx, AF.Exp), reads=["LS%d" % i], writes=["LS%d" % i])
            kb.op("dve", lambda e: e.tensor_tensor(NLAM, S2, S1, ALU.subtract), reads=["LS0", "LS1"], writes=["NLAM"])
            kb.op("dve", lambda e: e.tensor_scalar(NLAM, NLAM, -lam_init, None, ALU.add), reads=["NLAM"], writes=["NLAM"])
            kb.op("dve", lambda e: e.tensor_scalar(GSL, GSUB, 1.0 - lam_init, None, ALU.mult), reads=["GSUBr"], writes=["GSL"])
        else:
            NS = len(NA_SLOTS)
            BIAS = self.at("BIAS", [128, 2, NS, 128], BF16, o); o += 2 * NS * 256
            bias_d = self.din("na_bias", [16, 128, NS, 128], BF16)
            self.kb.dma("sp", "ldid", [(self.IDENT[:], self.din("ident", [128, 128], BF16)[:, :])], writes=["IDENT"])
        assert self.base + o <= self.top, o
        wv = w.rearrange("(k p) c -> p k c", p=128)
        G3 = GK.rearrange("(r f) t -> r f t", r=8)
        gk = ["KT_out0", "KT_out1", "V_out0", "V_out1", "GATHK", "GATHV"]
        GKT = [G3[r, 0:TL, :].rearrange("(f a) b -> f (a b)", a=2) for r in range(8)]
        GVT = [G3[r, TL:2 * TL, :] for r in range(8)]
        SB = [self.at("SBp%d" % i, [128, TL], BF16, o + i * 4096) for i in range(2)]; o += 8192
        assert self.base + o <= self.top, o
        if not diff:
            HK = self.at("HK", [128, 2, NCH, 256], BF16, o); o += 8192
            HV = self.at("HV", [128, 2, 2, D], BF16, o); o += 8192
            assert self.base + o <= self.top, o
            for side in range(2):
                cols = slice(TL - 256, TL) if side == 0 else slice(0, 256)
                for r in range(8):
                    w = self.SMALL[:, 18 + side * 8 + r:19 + side * 8 + r]
                    sbk = SB[r % 2][:, :].rearrange("p (m t) -> p m t", t=256)
                    sbv = SB[r % 2][:, :].rearrange("p (c f) -> p c f", f=D)
                    kb.dma("sp", "ldh%d" % (r % 2), [(sbk, GKT[r][:, cols].rearrange("(m p) t -> p m t", p=128))], reads=gk, writes=["SBp%d" % (r % 2)])
                    if r == 0:
                        kb.op("dve", lambda e, w=w, sbk=sbk, side=side: e.tensor_scalar(HK[:, side], sbk, w, None, ALU.mult),
                              reads=["SBp%d" % (r % 2), "SEL"], writes=["HK%d" % side])
                    else:
                        kb.op("dve", lambda e, w=w, sbk=sbk, side=side: e.scalar_tensor_tensor(HK[:, side], sbk, w, HK[:, side], ALU.mult, ALU.add),
                              reads=["SBp%d" % (r % 2), "SEL", "HK%d" % side], writes=["HK%d" % side])
                    kb.dma("sp", "ldh%d" % (r % 2), [(sbv, GVT[r][cols, :].rearrange("(c p) f -> p c f", p=128))], reads=gk, writes=["SBp%d" % (r % 2)])
                    if r == 0:
                        kb.op("dve", lambda e, w=w, sbv=sbv, side=side: e.tensor_scalar(HV[:, side], sbv, w, None, ALU.mult),
                              reads=["SBp%d" % (r % 2), "SEL"], writes=["HV%d" % side])
                    else:
                        kb.op("dve", lambda e, w=w, sbv=sbv, side=side: e.scalar_tensor_tensor(HV[:, side], sbv, w, HV[:, side], ALU.mult, ALU.add),
                              reads=["SBp%d" % (r % 2), "SEL", "HV%d" % side], writes=["HV%d" % side])
        xk = gk
        scale = 0.125
        qblocks = [b for b in enumerate(BLOCKS) if b[1][0] < TL or need_ctx]
        bankrr = [0]

        def pair(p):
            ws = p % 2
            whk = ["WP%d" % ws]
            vr = lambda ap: ap.rearrange("(c q) f -> q c f", q=128)
            pr = slice(p * 128, (p + 1) * 128)
            if diff:
                kq = ["KT"] + ["KTq%d" % q for q in range(4)]
                vq = ["V"] + ["Vq%d" % q for q in range(4)]
                kb.dma("sp", "ldkt", [(KT[:, 0:TC], kt_ctx[pr, :])] + [(KT[:, TC + q * TL:TC + (q + 1) * TL], GKT[q][pr, :]) for q in range(4)],
                       reads=xk, writes=kq)
                kb.dma("sp", "ldv", [(V[:, 0:2, :], vr(v_ctx[:, pr]))] + [(V[:, 2 + q * 16:2 + (q + 1) * 16, :], vr(GVT[q][:, pr])) for q in range(4)],
                       reads=xk, writes=vq)
                for q in range(4):
                    self.select2(KT[:, TC + q * TL:TC + (q + 1) * TL], 128, None, GKT[4 + q][pr, :], SB[0][:, :], "SBp0", "KTq%d" % q, gk)
                    self.select2(V[:, 2 + q * 16:2 + (q + 1) * 16, :], 128, None, vr(GVT[4 + q][:, pr]),
                                 SB[1][:, :].rearrange("p (c f) -> p c f", f=128), "SBp1", "Vq%d" % q, gk)
            else:
                kq = ["KT", "KTh"]
                vq = ["V", "Vh"]
                kb.dma("sp", "ldkt", [(KT[:, 0:TC], kt_ctx[pr, :]), (KT[:, TC + 256:TC + 256 + TL], kt_lat[pr, :])], reads=xk, writes=["KT"])
                kb.dma("sp", "ldv", [(V[:, 0:2, :], vr(v_ctx[:, pr])), (V[:, 4:20, :], vr(v_lat[:, pr]))], reads=xk, writes=["V"])
                kb.op("dve", lambda e: e.tensor_copy(KT[:, TC:TC + 256], HK[:, 0, p, :]), reads=["HK0", "KT"], writes=["KTh"])
                kb.op("dve", lambda e: e.tensor_copy(KT[:, TC + 256 + TL:TC + 512 + TL], HK[:, 1, p, :]), reads=["HK1", "KTh"], writes=["KTh"])
                kb.op("dve", lambda e: e.tensor_copy(V[:, 2:4, :], HV[:, 0, :, pr]), reads=["HV0", "V"], writes=["Vh"])
                kb.op("dve", lambda e: e.tensor_copy(V[:, 20:22, :], HV[:, 1, :, pr]), reads=["HV1", "Vh"], writes=["Vh"])
            kb.dma("pool", "ldwp%d" % ws, [(WQ[ws][:, k, :], wv[:, k, p * 128:(p + 1) * 128]) for k in range(NCH)] +
                   [(WOUT[ws][:, :], w_out[p * 128:(p + 1) * 128, :])], writes=whk)
            if diff:
                for k in range(NCH):
                    src = WQ[ws][:, k, :].rearrange("p (h t d) -> p h t d", t=2, d=32)
                    dst = WQS[ws][:, k, :].rearrange("p (h t d) -> p h t d", t=2, d=32)
                    kb.op("pool", lambda e, src=src, dst=dst: e.tensor_copy(dst[:, :, 0, :], src[:, :, 1, :]), reads=whk, writes=["WQSa%d_%d" % (ws, k)])
                    kb.op("pool", lambda e, src=src, dst=dst: e.tensor_copy(dst[:, :, 1, :], src[:, :, 0, :]), reads=whk, writes=["WQSb%d_%d" % (ws, k)])
                wqs_keys = ["WQSa%d_%d" % (ws, k) for k in range(NCH)] + ["WQSb%d_%d" % (ws, k) for k in range(NCH)]
            else:
                kb.dma("sp", "ldb", [(BIAS[:, hh, :, :], bias_d[2 * p + hh]) for hh in range(2)], writes=["BIAS"])

            def qproj(bi, t0, n):
                pa, pb = bi % 2, 2 + bi % 2
                kb.op("pe", [lambda e, k=k: e.matmul(self.PS[pa][:, :n], WQ[ws][:, k, :], self.H[:, k, t0:t0 + n], start=(k == 0), stop=(k == NCH - 1))
                             for k in range(NCH)], reads=whk + self.HK(bi), writes=["ps%d" % pa])
                if diff and t0 < TL:
                    kb.op("pe", [lambda e, k=k: e.matmul(self.PS[pb][:, :n], WQS[ws][:, k, :], self.H[:, k, t0:t0 + n], start=(k == 0), stop=(k == NCH - 1))
                                 for k in range(NCH)], reads=wqs_keys + self.HK(bi), writes=["ps%d" % pb])
                    kb.op("dve", lambda e: e.tensor_tensor(F[0][:, :n], self.PS[pa][:, :n], COS[:, t0:t0 + n], ALU.mult), reads=["ps%d" % pa, "TAB"], writes=["F0"])
                    kb.op("dve", lambda e: e.tensor_tensor(F[1][:, :n], self.PS[pb][:, :n], SIN[:, t0:t0 + n], ALU.mult), reads=["ps%d" % pb, "TAB"], writes=["F1"])
                    kb.op("pool", lambda e: e.tensor_tensor(QT[:, t0:t0 + n], F[0][:, :n], F[1][:, :n], ALU.add), reads=["F0", "F1"], writes=["QT%d" % bi])
                else:
                    kb.op("act", lambda e: e.activation(QT[:, t0:t0 + n], self.PS[pa][:, :n], AF.Copy, scale=(1.0 if diff else scale)),
                          reads=["ps%d" % pa], writes=["QT%d" % bi])

            for bi, (t0, n) in qblocks:
                qproj(bi, t0, n)

            def run_block(bi, q0, nq, chunks, jm):
                streams = []
                for hh in range(2):
                    def score(c, hh=hh):
                        mm = [(KT[hh * 64:(hh + 1) * 64, c[0] * 128:(c[0] + 1) * 128], QT[hh * 64:(hh + 1) * 64, q0:q0 + nq])]
                        if c[1] is not None:
                            mm.append((BIAS[:, hh, c[1], :], self.IDENT[:, :nq]))
                        return mm
                    if diff:
                        vf = lambda c: V[:, c[0], :]
                        M = 128
                    else:
                        vf = lambda c, hh=hh: V[:, c[0], hh * 64:(hh + 1) * 64]
                        M = 64
                    streams.append(dict(score=score, v=vf, M=M, st=(2 * hh, 2 * hh + 1), o=4 + hh, l=6 + hh,
                                        pt=(PT[2 * hh], PT[2 * hh + 1]), ptk=("PT%d" % (2 * hh), "PT%d" % (2 * hh + 1)),
                                        reads=kq + ["QT%d" % bi] + ([] if diff else ["BIAS", "IDENT"]), vreads=vq))
                self.attn(nq, chunks, streams, scale if diff else 1.0)
                sl = bankrr[0] % 2
                on = ON[sl]
                if diff:
                    kb.op("dve", lambda e: e.reciprocal(F[0][:, :nq], self.PS[6][:, :nq]), reads=["ps6"], writes=["F0"])
                    kb.op("dve", lambda e: e.reciprocal(F[1][:, :nq], self.PS[7][:, :nq]), reads=["ps7"], writes=["F1"])
                    kb.op("dve", lambda e: e.tensor_tensor(F[2][:, :nq], self.PS[4][:, :nq], F[0][:, :nq], ALU.mult), reads=["ps4", "F0"], writes=["F2"])
                    kb.op("dve", lambda e: e.tensor_tensor(F[3][:, :nq], self.PS[5][:, :nq], F[1][:, :nq], ALU.mult), reads=["ps5", "F1"], writes=["F3"])
                    kb.op("dve", lambda e: e.scalar_tensor_tensor(F[2][:, :nq], F[3][:, :nq], NLAM, F[2][:, :nq], ALU.mult, ALU.add),
                          reads=["F2", "F3", "NLAM"], writes=["F2"])
                    kb.op("act", lambda e: e.activation(SQd[:, :nq], F[2][:, :nq], AF.Square), reads=["F2"], writes=["SQd"])
                    kb.op("pe", lambda e: e.matmul(self.PS[0][:, :nq], self.ONES[:, :], SQd[:, :nq], start=True, stop=True), reads=["SQd", "ONES"], writes=["ps0"])
                    kb.op("act", lambda e: e.activation(F[0][:, :nq], self.PS[0][:, :nq], AF.Sqrt, bias=self.epsb(), scale=1.0 / 128), reads=["ps0", "EPSB"], writes=["F0"])
                    kb.op("dve", lambda e: e.reciprocal(F[0][:, :nq], F[0][:, :nq]), reads=["F0"], writes=["F0"])
                    kb.op("dve", lambda e: e.scalar_tensor_tensor(on[:, :nq], F[2][:, :nq], GSL, F[0][:, :nq], ALU.mult, ALU.mult),
                          reads=["F2", "F0", "GSL"], writes=["ON%d" % sl])
                else:
                    for hh in range(2):
                        kb.op("dve", lambda e, hh=hh: e.reciprocal(F[hh][:64, :nq], self.PS[6 + hh][:64, :nq]), reads=["ps%d" % (6 + hh)], writes=["F%d" % hh])
                        kb.op("dve", lambda e, hh=hh: e.tensor_tensor(on[hh * 64:(hh + 1) * 64, :nq], self.PS[4 + hh][:64, :nq], F[hh][:64, :nq], ALU.mult),
                              reads=["ps%d" % (4 + hh), "F%d" % hh], writes=["ON%d_%d" % (sl, hh)])
                onk = ["ON%d" % sl] if diff else ["ON%d_0" % sl, "ON%d_1" % sl]
                for m in range(NCH):
                    bank = bankrr[0] % 4
                    bankrr[0] += 1
                    kb.op("pe", lambda e, bank=bank, m=m: e.matmul(self.PS[bank][:, :nq], WOUT[ws][:, m * 128:(m + 1) * 128], on[:, :nq], start=True, stop=True),
                          reads=whk + onk, writes=["ps%d" % bank])
                    kb.op("dve", lambda e, bank=bank, m=m: e.scalar_tensor_tensor(
                        self.S[:, m, q0:q0 + nq], self.PS[bank][:, :nq], self.GT[:, 1, m, jm:jm + 1], self.S[:, m, q0:q0 + nq], ALU.mult, ALU.add),
                        reads=["ps%d" % bank, "GT"] + self.SK(bi, m), writes=self.SK(bi, m))

            if diff:
                for bi, (t0, n) in qblocks:
                    chunks = [(c, None) for c in range(nkc)] if t0 < TL else [(0, None), (1, None)]
                    run_block(bi, t0, n, chunks, 0 if t0 < TL else 1)
            else:
                for ga in range(16):
                    chunks = [(2 + ga + 2 + dl, NA_SLOTS.index((na_type(ga), dl))) for dl in na_deltas(ga)] + [(0, None), (1, None)]
                    run_block(ga // 4, ga * 128, 128, chunks, 0)
                if need_ctx:
                    run_block(4, TL, TC, [(0, None), (1, None)], 1)

        for p in range(8):
            pair(p)
        kb.barrier()

    def final_norm(self):
        kb = self.kb
        out = self.dout("outT", [D, TL], F32)
        a0 = self.abase
        FG = self.at("FG", [128, NCH], F32, a0)
        SQ = self.at("SQ", [128, NCH, 512], BF16, a0 + 64)
        RS = [self.at("RS%d" % i, [128, 512], F32, a0 + 8256 + i * 2048) for i in range(2)]
        OB = [self.at("OB%d" % i, [128, NCH, 512], F32, a0 + 12352 + i * 16384) for i in range(2)]
        self.small("final_g", [128, NCH], FG[:], "FG")
        ov = out.rearrange("(k p) t -> p k t", p=128)
        for bi, (t0, n) in enumerate(BLOCKS[:4]):
            rs = RS[bi % 2]
            rk = "RS%d" % (bi % 2)
            ob = OB[bi % 2]
            ok = "OB%d" % (bi % 2)
            bank_i = 6 + bi % 2
            bank = self.PS[bank_i]
            kb.op("act", lambda e, t0=t0, n=n: e.activation(SQ[:, :, :n], self.S[:, :, t0:t0 + n], AF.Square),
                  reads=self.SK(bi), writes=["SQ"])
            kb.op("pe", [lambda e, k=k, n=n, bank=bank: e.matmul(bank[:, :n], self.ONES[:, :], SQ[:, k, :n],
                                                                  start=(k == 0), stop=(k == NCH - 1)) for k in range(NCH)],
                  reads=["SQ", "ONES"], writes=["ps%d" % bank_i])
            kb.op("act", lambda e, n=n, bank=bank, rs=rs: e.activation(rs[:, :n], bank[:, :n], AF.Sqrt, bias=self.epsb(), scale=1.0 / D),
                  reads=["ps%d" % bank_i, "EPSB"], writes=[rk])
            kb.op("dve", lambda e, n=n, rs=rs: e.reciprocal(rs[:, :n], rs[:, :n]), reads=[rk], writes=[rk])
            for k in range(NCH):
                kb.op("dve", lambda e, k=k, t0=t0, n=n, rs=rs, ob=ob: e.scalar_tensor_tensor(
                    ob[:, k, :n], self.S[:, k, t0:t0 + n], FG[:, k:k + 1], rs[:, :n], ALU.mult, ALU.mult),
                    reads=self.SK(bi, k) + ["FG", rk], writes=[ok])
            kb.dma("sp", "sto%d" % (bi % 2), [(ov[:, k, t0:t0 + n], ob[:, k, :n]) for k in range(NCH)],
                   reads=[ok], writes=["outT"])

    def finish(self):
        self.kb.barrier()
        self.kb.finish()
        self.es.close()
        return self.nc


def _run(prog, in_maps):
    nc = prog.finish()
    for m in in_maps:
        assert set(m.keys()) == set(prog.inputs.keys()), (sorted(set(m.keys()) ^ set(prog.inputs.keys())))
    res = run_bass_kernel_spmd(nc, in_maps, core_ids=list(range(len(in_maps))))
    return res.results


def rope_tables(qr):
    t = np.arange(qr * TL, (qr + 1) * TL)
    row = (t // 64).astype(np.float32)
    col = (t % 64).astype(np.float32)
    inv = (10000.0 ** (-np.arange(0, 32, 2, dtype=np.float32) / 32)).astype(np.float32)
    ang = np.concatenate([row[:, None] * inv, col[:, None] * inv], axis=-1)
    cos = np.cos(ang).T.astype(np.float32)
    sin = np.sin(ang).T.astype(np.float32)
    return (np.ascontiguousarray(np.concatenate([cos] * 4, axis=0)),
            np.ascontiguousarray(np.concatenate([-sin, sin, -sin, sin], axis=0)))


def na_bias_tiles(rpb, qr):
    NS = len(NA_SLOTS)
    out = np.empty((16, 128, NS, 128), np.float32)
    i = np.arange(2)[:, None, None, None]
    c = np.arange(64)[None, :, None, None]
    u = np.arange(2)[None, None, :, None]
    kc = np.arange(64)[None, None, None, :]
    for si, (ga, dl) in enumerate(NA_SLOTS):
        a = qr * 16 + ga
        r = 2 * a + i
        kr = 2 * (a + dl) + u
        rs = np.clip(r - 4, 0, 120)
        cs = np.clip(c - 8, 0, 48)
        valid = (kr >= 0) & (kr < 128) & (kr >= rs) & (kr <= rs + 7) & (kc >= cs) & (kc <= cs + 15)
        ir = np.clip(kr - r + 7, 0, 14)
        ic = np.clip(kc - c + 15, 0, 30)
        valid = np.broadcast_to(valid, (2, 64, 2, 64))
        ir = np.broadcast_to(ir, (2, 64, 2, 64))
        ic = np.broadcast_to(ic, (2, 64, 2, 64))
        vals = rpb[:, ir, ic]
        vals = np.where(valid[None], vals, np.float32(-30000.0))
        out[:, :, si, :] = vals.reshape(16, 128, 128)
    return out.astype(ml_dtypes.bfloat16)


def build_fused(nl=DEPTH):
    p = Prog("fused")
    p.init_consts()
    p.init_small()
    p.load_state()
    for li in range(nl):
        p.modulation(li)
        p.derive_mod(li)
        p.kb.barrier()
        p.normmod(0)
        p.ffn(li, 0, 0)
        p.kb.barrier()
        p.normmod(1)
        need_ctx = li < DEPTH - 1
        if li % 3 == 0:
            p.mla_kv(li // 3)
            p.mla_attn(li // 3, need_ctx)
        elif li % 3 == 1:
            p.qkv_kv("b_w_qkv", True)
            p.pair_attn("b", li, need_ctx)
        else:
            p.qkv_kv("c_w_qkv", False)
            p.pair_attn("c", li, need_ctx)
        p.normmod(2)
        p.ffn(li, 1, 2)
        p.kb.barrier()
    p.final_norm()
    return p


def _provide(name, core, d, res, st):
    b, qr = core // 4, core % 4
    f32 = np.float32
    c = np.ascontiguousarray
    if name == "S_in":
        if st == 0:
            return c(np.concatenate([d["x"][b, qr * TL:(qr + 1) * TL], d["ctx"][b]], axis=0).T)
        return np.asarray(res[core]["S_out"])
    if name == "MOD_in":
        return np.asarray(res[core]["MOD_out"])
    if name.startswith("w_mod"):
        return d["w_mod"][int(name[5:])]
    if name.startswith("cT"):
        cp = np.stack([d["c"][b], d["c_ctx"]], axis=1)
        return c(cp.reshape(8, 128, 2).transpose(1, 0, 2))
    if name.startswith("b_mod"):
        bm = d["b_mod"][int(name[5:])].reshape(72, 128).T
        return c(np.repeat(bm[:, :, None], 2, axis=2))
    if name.startswith("norm_g"):
        ng = d["norm_g"][int(name[6:])].reshape(3, 8, 128).transpose(2, 0, 1)
        return c(np.repeat(ng[:, :, :, None], 2, axis=3))
    if name[:2] in ("wg", "wu", "wd"):
        li, f = int(name[2]), int(name[4])
        return d[{"wg": "w_ffn_gate", "wu": "w_ffn_up", "wd": "w_ffn_down"}[name[:2]]][li, f]
    for k in ("a_w_in", "a_w_uq", "a_w_ukv", "a_w_out"):
        if name.startswith(k) and name[len(k):].isdigit():
            return d[k][int(name[len(k):])]
    if name.startswith("a_kv_norm"):
        return c(d["a_kv_norm"][int(name[9:])].reshape(128, 1))
    if name.startswith("a_q_norm"):
        return c(d["a_q_norm"][int(name[8:])].reshape(2, 128).T)
    if name in ("cos4", "sinm4"):
        t = rope_tables(qr)
        return t[0] if name == "cos4" else t[1]
    if name == "KVL_all":
        kv = [np.asarray(res[b * 4 + q]["KVL_out"]) for q in range(4)]
        return c(np.concatenate([kv[qr][:, TL:]] + [k[:, :TL] for k in kv], axis=1))
    if name in ("b_w_qkv", "c_w_qkv", "b_w_out", "c_w_out"):
        return d[name][0]
    if name == "b_lam":
        lam = np.stack([d["b_lambda_q1"][0], d["b_lambda_k1"][0], d["b_lambda_q2"][0], d["b_lambda_k2"][0]], axis=0)
        return c(np.broadcast_to(lam[None], (128, 4, 64)))
    if name == "b_subln":
        return c(d["b_subln"][0].reshape(128, 1))
    if name in ("KT_all", "V_all"):
        kt = [np.asarray(res[b * 4 + q]["KT_out"]) for q in range(4)]
        vv = [np.asarray(res[b * 4 + q]["V_out"]) for q in range(4)]
        if st == 2:
            if name == "KT_all":
                return c(np.concatenate([kt[qr][:, TL:]] + [k[:, :TL] for k in kt], axis=1))
            return c(np.concatenate([vv[qr][TL:]] + [v[:TL] for v in vv], axis=0))
        lo, hi = (qr * 32 - 4) * 64, (qr * 32 + 36) * 64
        if name == "KT_all":
            full = np.concatenate([k[:, :TL] for k in kt], axis=1)
            band = np.zeros((D, hi - lo), full.dtype)
            a, e = max(lo, 0), min(hi, SEQ)
            band[:, a - lo:e - lo] = full[:, a:e]
            return c(np.concatenate([kt[qr][:, TL:], band], axis=1))
        full = np.concatenate([v[:TL] for v in vv], axis=0)
        band = np.zeros((hi - lo, D), full.dtype)
        a, e = max(lo, 0), min(hi, SEQ)
        band[a - lo:e - lo] = full[a:e]
        return c(np.concatenate([vv[qr][TL:], band], axis=0))
    if name == "na_bias":
        return na_bias_tiles(d["c_rpb"][0], qr)
    if name == "sel":
        v = np.zeros((128, 18), np.float32)
        v[:, b] = 1.0
        if qr > 0:
            v[:, 2 + core - 1] = 1.0
        if qr < 3:
            v[:, 10 + core + 1] = 1.0
        return v
    if name == "ident":
        return np.eye(128, dtype=f32).astype(ml_dtypes.bfloat16)
    if name == "final_g":
        return c(d["final_g"].reshape(8, 128).T)
    raise KeyError(name)


def kernel(**inp):
    d = {k: np.asarray(v) for k, v in inp.items()}
    p = build_fused()
    in_maps = [{name: _provide(name, core, d, None, 0) for name in p.inputs} for core in range(8)]
    res = _run(p, in_maps)
    out = np.empty((2, SEQ, D), np.float32)
    for core in range(8):
        b, qr = core // 4, core % 4
        out[b, qr * TL:(qr + 1) * TL] = np.asarray(res[core]["outT"]).T
    return out
```
